# Optimizing a Trainium2 kernel written in Bass

```python
import math
import jax, jax.numpy as jnp
from jax import lax
import numpy as np

D_MODEL = 2048
BATCH = 4
SEQ = 4096
DEPTH = 4

MIX_WIDTH = D_MODEL
HEAD_DIM = 128
MLA_HEADS = 8
SB_HEADS = 8
MLA_WIDTH = MLA_HEADS * HEAD_DIM
SB_WIDTH = SB_HEADS * HEAD_DIM
Q_LORA_RANK = 768
KV_LORA_RANK = 512
QK_NOPE_DIM = 128
QK_ROPE_DIM = 64
V_HEAD_DIM = HEAD_DIM
MLA_QK_DIM = QK_NOPE_DIM + QK_ROPE_DIM
ROPE_THETA = 10000.0
D_FF = 5632
BLOCK_Q = 128
N_ADA = 9
DEEPNORM_ALPHA = (2.0 * DEPTH) ** 0.25
DEEPNORM_BETA = (8.0 * DEPTH) ** -0.25
FFN_RESIDUAL_WEIGHT = 0.5
LN_EPS = 1e-5
RMS_EPS = 1e-6
MASK_VALUE = -1e30
IN_COLS = Q_LORA_RANK + KV_LORA_RANK + QK_ROPE_DIM + 3 * SB_WIDTH

kernel_name = "hybrid_mla_stickbreaking_macaron_deepnorm_adaln"


def layer_norm(x, g, b):
    xf = x.astype(jnp.float32)
    mu = jnp.mean(xf, axis=-1, keepdims=True)
    var = jnp.mean(jnp.square(xf - mu), axis=-1, keepdims=True)
    y = (xf - mu) * lax.rsqrt(var + LN_EPS)
    return (y * g.astype(jnp.float32) + b.astype(jnp.float32)).astype(x.dtype)


def rms_norm(x, g):
    xf = x.astype(jnp.float32)
    y = xf * lax.rsqrt(jnp.mean(xf * xf, axis=-1, keepdims=True) + RMS_EPS)
    return (y * g.astype(jnp.float32)).astype(x.dtype)


def modulate(x, shift, scale):
    return x * (1.0 + scale[:, None, :]) + shift[:, None, :]


def swiglu_ffn(h, wi, wo):
    g, u = jnp.split(h @ wi, 2, axis=-1)
    return (jax.nn.silu(g) * u) @ wo


def rope(x, positions):
    half = QK_ROPE_DIM // 2
    inv_freq = ROPE_THETA ** (-jnp.arange(half, dtype=jnp.float32) / half)
    ang = positions.astype(jnp.float32)[..., None] * inv_freq
    cos = jnp.cos(ang)[:, :, None, :]
    sin = jnp.sin(ang)[:, :, None, :]
    xf = x.astype(jnp.float32)
    x1, x2 = xf[..., :half], xf[..., half:]
    return jnp.concatenate([x1 * cos - x2 * sin, x1 * sin + x2 * cos], axis=-1).astype(x.dtype)


def sweep_query_blocks(block_fn, q):
    B, S, H, D = q.shape
    nb = S // BLOCK_Q
    qb = q.reshape(B, nb, BLOCK_Q, H, D).transpose(1, 0, 2, 3, 4)
    out = lax.map(lambda args: block_fn(args[0], args[1]), (qb, jnp.arange(nb)))
    Dv = out.shape[-1]
    return out.transpose(1, 0, 2, 3, 4).reshape(B, S, H * Dv)


def causal_softmax_attention(q, k, v):
    S = k.shape[1]
    scale = q.shape[-1] ** -0.5
    k_idx = jnp.arange(S)

    def block(q_blk, blk):
        q_idx = blk * BLOCK_Q + jnp.arange(BLOCK_Q)
        s = jnp.einsum('bthd,bshd->bhts', q_blk, k).astype(jnp.float32) * scale
        mask = k_idx[None, :] <= q_idx[:, None]
        p = jax.nn.softmax(jnp.where(mask, s, MASK_VALUE), axis=-1).astype(v.dtype)
        return jnp.einsum('bhts,bshd->bthd', p, v)

    return sweep_query_blocks(block, q)


def stick_breaking_attention(q, k, v):
    S = k.shape[1]
    scale = q.shape[-1] ** -0.5
    k_idx = jnp.arange(S)

    def block(q_blk, blk):
        q_idx = blk * BLOCK_Q + jnp.arange(BLOCK_Q)
        z = jnp.einsum('bthd,bshd->bhts', q_blk, k).astype(jnp.float32) * scale
        mask = k_idx[None, :] < q_idx[:, None]
        log_beta = jax.nn.log_sigmoid(z)
        log_one_minus = jnp.where(mask, jax.nn.log_sigmoid(-z), 0.0)
        key_axis = log_one_minus.ndim - 1
        tail = lax.cumsum(log_one_minus, axis=key_axis, reverse=True) - log_one_minus
        a = jnp.where(mask, jnp.exp(log_beta + tail), 0.0).astype(v.dtype)
        return jnp.einsum('bhts,bshd->bthd', a, v)

    return sweep_query_blocks(block, q)


def hybrid_token_mixer(h, positions, w_in, q_norm_g, kv_norm_g, w_uq, w_ukv,
                       mla_out_g, sb_out_g, w_o):
    B, S, _ = h.shape
    proj = h @ w_in
    o1 = Q_LORA_RANK
    o2 = o1 + KV_LORA_RANK
    o3 = o2 + QK_ROPE_DIM
    c_q, c_kv, k_rope, sb_qkv = jnp.split(proj, [o1, o2, o3], axis=-1)

    q = (rms_norm(c_q, q_norm_g) @ w_uq).reshape(B, S, MLA_HEADS, MLA_QK_DIM)
    q_mla = jnp.concatenate([q[..., :QK_NOPE_DIM], rope(q[..., QK_NOPE_DIM:], positions)], axis=-1)
    kv = (rms_norm(c_kv, kv_norm_g) @ w_ukv).reshape(B, S, MLA_HEADS, QK_NOPE_DIM + V_HEAD_DIM)
    k_nope, v_mla = kv[..., :QK_NOPE_DIM], kv[..., QK_NOPE_DIM:]
    k_pe = rope(k_rope[:, :, None, :], positions)
    k_mla = jnp.concatenate(
        [k_nope, jnp.broadcast_to(k_pe, (B, S, MLA_HEADS, QK_ROPE_DIM))], axis=-1)
    o_mla = causal_softmax_attention(q_mla, k_mla, v_mla)

    sb_qkv = sb_qkv.reshape(B, S, 3, SB_HEADS, HEAD_DIM)
    o_sb = stick_breaking_attention(sb_qkv[:, :, 0], sb_qkv[:, :, 1], sb_qkv[:, :, 2])

    o = jnp.concatenate([rms_norm(o_mla, mla_out_g), rms_norm(o_sb, sb_out_g)], axis=-1)
    return o @ w_o


def setup_inputs(seed: int = 0) -> dict:
    key = jax.random.key(seed)
    ks = jax.random.split(key, 24)
    L, D, F = DEPTH, D_MODEL, D_FF

    def normal(k, shape, std):
        return jax.random.normal(k, shape, dtype=jnp.float32) * std

    x = normal(ks[0], (BATCH, SEQ, D), 1.0)
    c = normal(ks[1], (BATCH, D), 1.0)
    start = jax.random.randint(ks[2], (BATCH, 1), 0, 1024, dtype=jnp.int32)
    positions = (start + jnp.arange(SEQ, dtype=jnp.int32)[None, :]).astype(jnp.int32)

    ada_w = normal(ks[3], (L, D, N_ADA * D), 0.1 * D ** -0.5)
    ada_b = normal(ks[4], (L, N_ADA * D), 0.01)
    ln_g = 1.0 + normal(ks[5], (L, 3, D), 0.02)
    ln_b = normal(ks[6], (L, 3, D), 0.02)
    ffn1_wi = normal(ks[7], (L, D, 2 * F), D ** -0.5)
    ffn1_wo = normal(ks[8], (L, F, D), DEEPNORM_BETA * F ** -0.5)
    w_in = normal(ks[9], (L, D, IN_COLS), D ** -0.5)
    q_norm_g = 1.0 + normal(ks[10], (L, Q_LORA_RANK), 0.02)
    kv_norm_g = 1.0 + normal(ks[11], (L, KV_LORA_RANK), 0.02)
    w_uq = normal(ks[12], (L, Q_LORA_RANK, MLA_HEADS * MLA_QK_DIM), Q_LORA_RANK ** -0.5)
    w_ukv = normal(ks[13], (L, KV_LORA_RANK, MLA_HEADS * (QK_NOPE_DIM + V_HEAD_DIM)), KV_LORA_RANK ** -0.5)
    mla_out_g = 1.0 + normal(ks[14], (L, MLA_WIDTH), 0.02)
    sb_out_g = 1.0 + normal(ks[15], (L, SB_WIDTH), 0.02)
    w_o = normal(ks[16], (L, MIX_WIDTH, D), DEEPNORM_BETA * MIX_WIDTH ** -0.5)
    ffn2_wi = normal(ks[17], (L, D, 2 * F), D ** -0.5)
    ffn2_wo = normal(ks[18], (L, F, D), DEEPNORM_BETA * F ** -0.5)
    return {"x": x, "c": c, "positions": positions,
            "ada_w": ada_w, "ada_b": ada_b, "ln_g": ln_g, "ln_b": ln_b,
            "ffn1_wi": ffn1_wi, "ffn1_wo": ffn1_wo,
            "w_in": w_in, "q_norm_g": q_norm_g, "kv_norm_g": kv_norm_g,
            "w_uq": w_uq, "w_ukv": w_ukv, "mla_out_g": mla_out_g, "sb_out_g": sb_out_g,
            "w_o": w_o, "ffn2_wi": ffn2_wi, "ffn2_wo": ffn2_wo}


def reference(x, c, positions, ada_w, ada_b, ln_g, ln_b, ffn1_wi, ffn1_wo,
              w_in, q_norm_g, kv_norm_g, w_uq, w_ukv, mla_out_g, sb_out_g,
              w_o, ffn2_wi, ffn2_wo):
    c_act = jax.nn.silu(c)
    for l in range(DEPTH):
        ada = c_act @ ada_w[l] + ada_b[l]
        sh1, sc1, g1, shm, scm, gm, sh2, sc2, g2 = jnp.split(ada, N_ADA, axis=-1)

        f = swiglu_ffn(modulate(x, sh1, sc1), ffn1_wi[l], ffn1_wo[l])
        x = layer_norm(DEEPNORM_ALPHA * x + FFN_RESIDUAL_WEIGHT * (1.0 + g1)[:, None, :] * f,
                       ln_g[l, 0], ln_b[l, 0])

        m = hybrid_token_mixer(modulate(x, shm, scm), positions, w_in[l], q_norm_g[l], kv_norm_g[l],
                               w_uq[l], w_ukv[l], mla_out_g[l], sb_out_g[l], w_o[l])
        x = layer_norm(DEEPNORM_ALPHA * x + (1.0 + gm)[:, None, :] * m, ln_g[l, 1], ln_b[l, 1])

        f = swiglu_ffn(modulate(x, sh2, sc2), ffn2_wi[l], ffn2_wo[l])
        x = layer_norm(DEEPNORM_ALPHA * x + FFN_RESIDUAL_WEIGHT * (1.0 + g2)[:, None, :] * f,
                       ln_g[l, 2], ln_b[l, 2])
    return x
```

```python
import numpy as np
import ml_dtypes
from contextlib import ExitStack

import concourse.bass as bass
import concourse.mybir as mybir
from concourse.bass_utils import run_bass_kernel_spmd

F32 = mybir.dt.float32
BF16 = mybir.dt.bfloat16
I32 = mybir.dt.int32
AF = mybir.ActivationFunctionType
ALU = mybir.AluOpType


class Tok:
    __slots__ = ("eng", "needed", "val", "sem", "is_mm")

    def __init__(self, eng):
        self.eng = eng
        self.needed = False
        self.val = None
        self.sem = None
        self.is_mm = False


class Buf:
    __slots__ = ("name", "last_w", "readers")

    def __init__(self, name=""):
        self.name = name
        self.last_w = None
        self.readers = []


N_DMA_SEMS = 8
ENGINES = ("pe", "act", "dve", "pool", "sp")


class Prog:
    def __init__(self, nc):
        self.nc = nc
        self.lists = {e: [] for e in ENGINES}
        self.dma_count = {e: 0 for e in ENGINES}
        self.dma_hist = {e: [] for e in ENGINES}
        self.out_dma_toks = []

    def _deps(self, eng, reads, writes, is_mm):
        deps = []
        for b in reads:
            if b.last_w is not None:
                deps.append(b.last_w)
        for b in writes:
            if b.last_w is not None:
                deps.append(b.last_w)
            deps.extend(b.readers)
        out = []
        seen = set()
        for d in deps:
            if id(d) in seen:
                continue
            seen.add(id(d))
            if is_mm and d.is_mm:
                continue
            out.append(d)
        return out

    def _commit(self, tok, reads, writes):
        for b in reads:
            b.readers.append(tok)
        for b in writes:
            b.last_w = tok
            b.readers = []

    def op(self, eng, fn, reads=(), writes=(), is_mm=False):
        tok = Tok(eng)
        tok.is_mm = is_mm
        deps = self._deps(eng, reads, writes, is_mm)
        for d in deps:
            d.needed = True
        self.lists[eng].append(("op", fn, deps, tok))
        self._commit(tok, reads, writes)
        return tok

    def dma(self, eng, fn, reads=(), writes=(), is_output=False):
        tok = Tok(eng)
        tok.needed = True
        deps = self._deps(eng, reads, writes, False)
        for d in deps:
            d.needed = True
        i = self.dma_count[eng]
        self.dma_count[eng] += 1
        prev = self.dma_hist[eng][i - N_DMA_SEMS] if i >= N_DMA_SEMS else None
        self.dma_hist[eng].append(tok)
        self.lists[eng].append(("dma", fn, deps, tok, i, prev))
        self._commit(tok, reads, writes)
        if is_output:
            self.out_dma_toks.append(tok)
        return tok

    def emit(self, es):
        nc = self.nc
        sems = {e: es.enter_context(nc.semaphore("s_" + e)) for e in ENGINES}
        dsems = {e: [es.enter_context(nc.semaphore("d_%s%d" % (e, k))) for k in range(N_DMA_SEMS)]
                 for e in ("sp", "pool", "act")}
        for e in ENGINES:
            cnt = 0
            for rec in self.lists[e]:
                tok = rec[3]
                if rec[0] == "op":
                    if tok.needed:
                        cnt += 1
                        tok.sem = sems[e]
                        tok.val = cnt
                else:
                    i = rec[4]
                    tok.sem = dsems[e][i % N_DMA_SEMS]
                    tok.val = 16 * (i // N_DMA_SEMS + 1)
        final_waits = list(self.out_dma_toks)
        lists = self.lists

        def run(e, engobj):
            waited = {}

            def wait(tok):
                key = id(tok.sem)
                if waited.get(key, 0) >= tok.val:
                    return
                waited[key] = tok.val
                engobj.wait_ge(tok.sem, tok.val)

            for rec in lists[e]:
                if rec[0] == "op":
                    _, fn, deps, tok = rec
                    for d in deps:
                        wait(d)
                    ins = fn(engobj)
                    if tok.needed:
                        ins.then_inc(tok.sem, 1)
                else:
                    _, fn, deps, tok, i, prev = rec
                    if prev is not None:
                        wait(prev)
                    for d in deps:
                        wait(d)
                    fn(engobj).then_inc(tok.sem, 16)
            if e == "sp":
                for t in final_waits:
                    wait(t)

        with nc.Block() as block:
            @block.tensor
            def _(t):
                run("pe", t)

            @block.scalar
            def _(t):
                run("act", t)

            @block.vector
            def _(t):
                run("dve", t)

            @block.gpsimd
            def _(t):
                run("pool", t)

            @block.sync
            def _(t):
                run("sp", t)


D = 2048
KC = 16
FF = 5632
FC = 44
TT = 512
NTOK = 2048
NTILE = NTOK // TT
SEQ = 4096
DEPTH = 4
NB = 4
ALPHA = 8.0 ** 0.25
LN_EPS_EFF = 1e-5 / (ALPHA * ALPHA)
RMS_EPS = 1e-6
QLR = 768
KVLR = 512
MLA_SCALE = 192.0 ** -0.5
SB_SCALE = 128.0 ** -0.5
AX = mybir.AxisListType


class Rot:
    def __init__(self, items):
        self.items = list(items)
        self.i = 0

    def next(self):
        it = self.items[self.i % len(self.items)]
        self.i += 1
        return it


def bcast_mid(ap, n):
    return bass.AP(ap.tensor, ap.offset, [ap.ap[0], [0, n], ap.ap[1]])


class KB:
    def __init__(self, nc, es, slab_elems=8192, nslab=3):
        self.nc = nc
        self.es = es
        self.P = Prog(nc)
        self.banks = [(es.enter_context(nc.psum_tensor("pb%d" % i, [128, 512], F32)), Buf("pb%d" % i))
                      for i in range(8)]
        self.slabs = Rot([(self.sb("slab%d" % i, [128, slab_elems], BF16), Buf("slab%d" % i))
                          for i in range(nslab)])
        self.dram = {}

    def sb(self, name, shape, dt):
        return self.es.enter_context(self.nc.sbuf_tensor("sb_" + name, list(shape), dt))

    def din(self, name, shape, dt):
        t = self.nc.dram_tensor(name, list(shape), dt, kind="ExternalInput").ap()
        self.dram[name] = t
        return t

    def dout(self, name, shape, dt):
        t = self.nc.dram_tensor(name, list(shape), dt, kind="ExternalOutput").ap()
        self.dram[name] = t
        return t

    def load_slab(self, src_ap, a, b):
        t, buf = self.slabs.next()
        view = t[:, 0:a * b].rearrange("p (a b) -> p a b", a=a)
        self.P.dma("pool", lambda e: e.dma_start(out=view, in_=src_ap, max_dma_last_dim=8192), writes=[buf])
        return view, buf

    def mm(self, bank, lhsT, rhs, reads, start, stop, out=None):
        t, b = bank
        o = t[:] if out is None else out
        self.P.op("pe", lambda e: e.matmul(o, lhsT, rhs, start=start, stop=stop),
                  reads=reads, writes=[b], is_mm=True)

    def act(self, out, in_, func, reads, writes, **kw):
        return self.P.op("act", lambda e: e.activation(out=out, in_=in_, func=func, **kw), reads, writes)

    def tt(self, out, in0, in1, op, reads, writes, eng="dve"):
        return self.P.op(eng, lambda e: e.tensor_tensor(out=out, in0=in0, in1=in1, op=op), reads, writes)

    def stt(self, out, in0, scalar, in1, op0, op1, reads, writes):
        return self.P.op("dve", lambda e: e.scalar_tensor_tensor(out=out, in0=in0, scalar=scalar, in1=in1,
                                                                  op0=op0, op1=op1), reads, writes)

    def ts(self, out, in0, s1, s2, op0, op1, reads, writes, eng="dve"):
        return self.P.op(eng, lambda e: e.tensor_scalar(out=out, in0=in0, scalar1=s1, scalar2=s2,
                                                        op0=op0, op1=op1), reads, writes)

    def copy(self, out, in_, reads, writes, eng="dve"):
        return self.P.op(eng, lambda e: e.tensor_copy(out=out, in_=in_), reads, writes)

    def recip(self, out, in_, reads, writes):
        return self.P.op("dve", lambda e: e.reciprocal(out=out, in_=in_), reads, writes)

    def red(self, out, in_, reads, writes):
        return self.P.op("dve", lambda e: e.tensor_reduce(out=out, in_=in_, axis=AX.X, op=ALU.add), reads, writes)

    def dma_in(self, out, in_, writes, reads=(), q="sp"):
        return self.P.dma(q, lambda e: e.dma_start(out=out, in_=in_), reads=reads, writes=writes)

    def dma_out(self, out, in_, reads, q="sp"):
        return self.P.dma(q, lambda e: e.dma_start(out=out, in_=in_), reads=reads, is_output=True)


class TokStage:
    def __init__(self, k):
        self.k = k
        self.xt = k.sb("xt", [128, KC, TT], F32)
        self.xb = [Buf("x%d" % i) for i in range(KC)]
        self.ht = k.sb("ht", [128, KC, TT], BF16)
        self.hb = [Buf("h%d" % i) for i in range(KC)]
        self.at = k.sb("at", [128, FC * TT], BF16)
        self.at3 = self.at[:].rearrange("p (c t) -> p c t", c=FC)
        self.atf = self.at.bitcast(F32)
        self.ab = [Buf("a%d" % i) for i in range(FC)]
        self.sg = Rot([(k.sb("sg%d" % i, [128, TT], F32), Buf("sg%d" % i)) for i in range(2)])
        self.stat = {n: (k.sb("st_" + n, [128, TT], F32), Buf("st_" + n))
                     for n in ("s1", "s2", "mean", "tmp", "rstd", "nmr", "r2")}
        self.ones = k.sb("ones", [128, 128], F32)
        self.onesb = Buf("ones")
        k.P.op("dve", lambda e: e.memset(self.ones[:], 1.0), writes=[self.onesb])
        self.ada_raw = k.sb("ada_raw", [128, 9, KC], F32)
        self.ada = k.sb("ada", [128, 9, KC], F32)
        self.adab = Buf("ada")
        self.lnp = k.sb("lnp", [128, 3, 2, KC], F32)
        self.lnb = Buf("lnp")

    def load_params(self, ada_d, lnp_d):
        k = self.k
        rb = Buf("ada_raw")
        k.dma_in(self.ada_raw[:], ada_d, writes=[rb])
        k.dma_in(self.lnp[:], lnp_d, writes=[self.lnb])
        for j in range(9):
            r = j % 3
            if r == 0:
                add, mul = 0.0, 1.0
            elif r == 1:
                add, mul = 1.0, 1.0
            else:
                add = 1.0
                mul = (1.0 if j == 5 else 0.5) / ALPHA
            k.ts(self.ada[:, j, :], self.ada_raw[:, j, :], add, mul, ALU.add, ALU.mult,
                 reads=[rb], writes=[self.adab])

    def modulate(self, jsh, jsc):
        k = self.k
        for c in range(KC):
            k.act(self.ht[:, c, :], self.xt[:, c, :], AF.Identity,
                  reads=[self.xb[c], self.adab], writes=[self.hb[c]],
                  scale=self.ada[:, jsc, c:c + 1], bias=self.ada[:, jsh, c:c + 1])

    def stats_rstd(self, src_sum_ap, n_feat, eps, reads, out_name):
        k = self.k
        bank = k.banks[7]
        k.mm(bank, self.ones[:], src_sum_ap, reads=list(reads) + [self.onesb], start=True, stop=True)
        t, b = self.stat[out_name]
        k.act(t[:], bank[0][:], AF.Sqrt, reads=[bank[1]], writes=[b], scale=1.0 / n_feat, bias=eps)
        k.recip(t[:], t[:], reads=[b], writes=[b])
        return t, b

    def layernorm(self, n):
        k = self.k
        xt, xb = self.xt, self.xb
        sq = self.atf[:, 0:KC * TT].rearrange("p (c t) -> p c t", c=KC)
        s1, s1b = self.stat["s1"]
        s2, s2b = self.stat["s2"]
        mean, meanb = self.stat["mean"]
        tmp, tmpb = self.stat["tmp"]
        rstd, rstdb = self.stat["rstd"]
        nmr, nmrb = self.stat["nmr"]
        k.red(s1[:], xt[:].rearrange("p c t -> p t c"), reads=xb, writes=[s1b])
        k.act(sq, xt[:], AF.Square, reads=xb, writes=self.ab)
        k.red(s2[:], sq.rearrange("p c t -> p t c"), reads=self.ab, writes=[s2b])
        b6, b7 = k.banks[6], k.banks[7]
        k.mm(b6, self.ones[:], s1[:], reads=[s1b, self.onesb], start=True, stop=True)
        k.mm(b7, self.ones[:], s2[:], reads=[s2b, self.onesb], start=True, stop=True)
        k.act(mean[:], b6[0][:], AF.Identity, reads=[b6[1]], writes=[meanb], scale=1.0 / D)
        k.tt(tmp[:], mean[:], mean[:], ALU.mult, reads=[meanb], writes=[tmpb])
        k.stt(rstd[:], b7[0][:], 1.0 / D, tmp[:], ALU.mult, ALU.subtract, reads=[b7[1], tmpb], writes=[rstdb])
        k.act(rstd[:], rstd[:], AF.Sqrt, reads=[rstdb], writes=[rstdb], bias=LN_EPS_EFF)
        k.recip(rstd[:], rstd[:], reads=[rstdb], writes=[rstdb])
        k.stt(nmr[:], mean[:], -1.0, rstd[:], ALU.mult, ALU.mult, reads=[meanb, rstdb], writes=[nmrb])
        k.tt(xt[:], xt[:], bcast_mid(rstd[:], KC), ALU.mult, reads=xb + [rstdb], writes=xb)
        k.tt(xt[:], xt[:], bcast_mid(nmr[:], KC), ALU.add, reads=xb + [nmrb], writes=xb)
        for c in range(KC):
            k.act(xt[:, c, :], xt[:, c, :], AF.Identity, reads=[xb[c], self.lnb], writes=[xb[c]],
                  scale=self.lnp[:, n, 0, c:c + 1], bias=self.lnp[:, n, 1, c:c + 1])

    def ffn(self, wi_d, wo_d, jsh, jsc, jg):
        k = self.k
        self.modulate(jsh, jsc)
        gb = Rot([k.banks[0], k.banks[1]])
        ub = Rot([k.banks[2], k.banks[3]])
        fb = Rot([k.banks[4], k.banks[5]])
        for j in range(FC):
            wv, wb = k.load_slab(wi_d[j], KC, 256)
            g = gb.next()
            u = ub.next()
            for kc in range(KC):
                k.mm(g, wv[:, kc, 0:128], self.ht[:, kc, :], reads=[wb, self.hb[kc]],
                     start=(kc == 0), stop=(kc == KC - 1))
            for kc in range(KC):
                k.mm(u, wv[:, kc, 128:256], self.ht[:, kc, :], reads=[wb, self.hb[kc]],
                     start=(kc == 0), stop=(kc == KC - 1))
            sgt, sgb = self.sg.next()
            k.act(sgt[:], g[0][:], AF.Silu, reads=[g[1]], writes=[sgb])
            k.tt(self.at3[:, j, :], sgt[:], u[0][:], ALU.mult, reads=[sgb, u[1]], writes=[self.ab[j]])
        for i in range(KC):
            wv, wb = k.load_slab(wo_d[i], FC, 128)
            f = fb.next()
            for j in range(FC):
                k.mm(f, wv[:, j, :], self.at3[:, j, :], reads=[wb, self.ab[j]],
                     start=(j == 0), stop=(j == FC - 1))
            k.stt(self.xt[:, i, :], f[0][:], self.ada[:, jg, i:i + 1], self.xt[:, i, :], ALU.mult, ALU.add,
                  reads=[f[1], self.adab, self.xb[i]], writes=[self.xb[i]])


TWO_PI = 2.0 * np.pi
CW1 = 6.28125
CW2 = TWO_PI - CW1


def rope_tables(k, pos_d, invf_d):
    nt = NTOK
    cw = 512
    posi = k.sb("posi", [64, cw], I32)
    ang = k.sb("ang", [64, cw], F32)
    kf = k.sb("kf", [64, cw], F32)
    ki = k.sb("ki", [64, cw], I32)
    cos2 = k.sb("cos2", [64, nt], F32)
    sin2 = k.sb("sin2", [64, nt], F32)
    invf = k.sb("invf", [64, 1], F32)
    sgn = k.sb("sgn", [64, 1], F32)
    b = Buf("rope")
    pa = pos_d
    k.dma_in(invf[:], invf_d, writes=[b])
    k.P.op("dve", lambda e: e.memset(sgn[0:32, :], -1.0), writes=[b])
    k.P.op("dve", lambda e: e.memset(sgn[32:64, :], 1.0), writes=[b])

    def wrap(t):
        k.ts(kf[:], t[:], -np.pi, TWO_PI, ALU.is_lt, ALU.mult, reads=[b], writes=[b])
        k.tt(t[:], t[:], kf[:], ALU.add, reads=[b], writes=[b])
        k.ts(kf[:], t[:], np.pi, -TWO_PI, ALU.is_gt, ALU.mult, reads=[b], writes=[b])
        k.tt(t[:], t[:], kf[:], ALU.add, reads=[b], writes=[b])

    for c0 in range(0, nt, cw):
        csl = slice(c0, c0 + cw)
        k.dma_in(posi[:], bass.AP(pa.tensor, pa.offset + c0, [[0, 64], [1, cw]]), writes=[b])
        k.copy(ang[:], posi[:], reads=[b], writes=[b])
        k.ts(ang[:], ang[:], invf[:, 0:1], None, ALU.mult, ALU.bypass, reads=[b], writes=[b])
        k.ts(kf[:], ang[:], 1.0 / TWO_PI, 0.5, ALU.mult, ALU.add, reads=[b], writes=[b])
        k.copy(ki[:], kf[:], reads=[b], writes=[b])
        k.copy(kf[:], ki[:], reads=[b], writes=[b])
        k.stt(ang[:], kf[:], -CW1, ang[:], ALU.mult, ALU.add, reads=[b], writes=[b])
        k.stt(ang[:], kf[:], -CW2, ang[:], ALU.mult, ALU.add, reads=[b], writes=[b])
        wrap(ang)
        k.act(sin2[:, csl], ang[:], AF.Sin, reads=[b], writes=[b])
        k.ts(sin2[:, csl], sin2[:, csl], sgn[:, 0:1], None, ALU.mult, ALU.bypass, reads=[b], writes=[b])
        k.ts(ang[:], ang[:], np.pi / 2, None, ALU.add, ALU.bypass, reads=[b], writes=[b])
        wrap(ang)
        k.act(cos2[:, csl], ang[:], AF.Sin, reads=[b], writes=[b])
    return cos2, sin2, b


def build_s1(ntile=NTILE):
    nc = bass.Bass("TRN2", target_bir_lowering=False)
    with ExitStack() as es:
        k = KB(nc, es)
        xT = k.din("xT", [128, KC, NTOK], F32)
        ada = k.din("ada", [128, 9, KC], F32)
        lnp = k.din("lnp", [128, 3, 2, KC], F32)
        wi = k.din("wi", [FC, 128, KC, 256], F32)
        wo = k.din("wo", [KC, 128, FC, 128], F32)
        w_inA = k.din("w_inA", [27, 128, KC, 128], F32)
        w_inV = k.din("w_inV", [2, 128, KC, 512], F32)
        w_uq = k.din("w_uq", [16, 128, 6, 128], F32)
        w_ukvK = k.din("w_ukvK", [8, 128, 4, 128], F32)
        w_ukvV = k.din("w_ukvV", [2, 128, 4, 512], F32)
        gq = k.din("gq", [128, 6], F32)
        gkv = k.din("gkv", [128, 4], F32)
        pos = k.din("pos", [NTOK], I32)
        invf = k.din("invf", [64, 1], F32)
        x1T = k.dout("x1T", [128, KC, NTOK], F32)
        QN = k.dout("QN", [128, 8, NTOK], BF16)
        QPE = k.dout("QPE", [64, 8, NTOK], BF16)
        KN = k.dout("KN", [128, 8, NTOK], BF16)
        KPE = k.dout("KPE", [64, NTOK], BF16)
        VM = k.dout("VM", [NTOK, 1024], BF16)
        SQ = k.dout("SQ", [128, 8, NTOK], BF16)
        SK = k.dout("SK", [128, 8, NTOK], BF16)
        SV = k.dout("SV", [NTOK, 1024], BF16)

        st = TokStage(k)
        st.load_params(ada, lnp)
        gq_s = k.sb("gq_s", [128, 6], F32)
        gkv_s = k.sb("gkv_s", [128, 4], F32)
        gb_ = Buf("gqkv")
        k.dma_in(gq_s[:], gq, writes=[gb_])
        k.dma_in(gkv_s[:], gkv, writes=[gb_])
        cos2, sin2, ropeb = rope_tables(k, pos, invf)
        cqn = k.sb("cqn", [128, 10, TT], BF16)
        cqnb = [Buf("cqn%d" % i) for i in range(10)]
        stage = Rot([(k.sb("stg%d" % i, [128, TT], BF16), Buf("stg%d" % i)) for i in range(4)])
        rtmp = Rot([(k.sb("rt%d" % i, [64, TT], F32), Buf("rt%d" % i)) for i in range(2)])
        bank_rot = Rot(k.banks[0:6])

        def evac_store(bank, dst_ap, npart=128, use_act=True):
            stg, sb_ = stage.next()
            if use_act:
                k.act(stg[0:npart, :], bank[0][0:npart, :], AF.Identity, reads=[bank[1]], writes=[sb_])
            else:
                k.copy(stg[0:npart, :], bank[0][0:npart, :], reads=[bank[1]], writes=[sb_])
            k.dma_out(dst_ap, stg[0:npart, :], reads=[sb_])

        def rope_store(slab, sbuf_, nk, rhs_t, rhs_bufs, tsl, dst_ap):
            b1 = bank_rot.next()
            b2 = bank_rot.next()
            for kc in range(nk):
                k.mm(b1, slab[:, kc, 0:64], rhs_t[:, kc, :], reads=[sbuf_, rhs_bufs[kc]],
                     start=(kc == 0), stop=(kc == nk - 1), out=b1[0][0:64, :])
            for kc in range(nk):
                k.mm(b2, slab[:, kc, 64:128], rhs_t[:, kc, :], reads=[sbuf_, rhs_bufs[kc]],
                     start=(kc == 0), stop=(kc == nk - 1), out=b2[0][0:64, :])
            t1, t1b = rtmp.next()
            t2, t2b = rtmp.next()
            k.tt(t1[:], b1[0][0:64, :], cos2[:, tsl], ALU.mult, reads=[b1[1], ropeb], writes=[t1b])
            k.tt(t2[:], b2[0][0:64, :], sin2[:, tsl], ALU.mult, reads=[b2[1], ropeb], writes=[t2b])
            stg, sb_ = stage.next()
            k.tt(stg[0:64, :], t1[:], t2[:], ALU.add, reads=[t1b, t2b], writes=[sb_])
            k.dma_out(dst_ap, stg[0:64, :], reads=[sb_])

        for t in range(ntile):
            tsl = slice(t * TT, (t + 1) * TT)
            k.dma_in(st.xt[:], xT[:, :, tsl], writes=st.xb)
            st.ffn(wi, wo, 0, 1, 2)
            st.layernorm(0)
            k.dma_out(x1T[:, :, tsl], st.xt[:], reads=st.xb)
            st.modulate(3, 4)
            ht, hb = st.ht, st.hb
            cq = st.atf[:, 0:10 * TT].rearrange("p (c t) -> p c t", c=10)
            sq = st.atf[:, 10 * TT:20 * TT].rearrange("p (c t) -> p c t", c=10)
            for s in range(10):
                slab, sbuf_ = k.load_slab(w_inA[s], KC, 128)
                bk = bank_rot.next()
                for kc in range(KC):
                    k.mm(bk, slab[:, kc, :], ht[:, kc, :], reads=[sbuf_, hb[kc]], start=(kc == 0), stop=(kc == KC - 1))
                k.act(cq[:, s, :], bk[0][:], AF.Identity, reads=[bk[1]], writes=st.ab)
            k.act(sq, cq, AF.Square, reads=st.ab, writes=st.ab)
            s2, s2b = st.stat["s2"]
            tmp, tmpb = st.stat["tmp"]
            k.red(s2[:], sq[:, 0:6, :].rearrange("p c t -> p t c"), reads=st.ab, writes=[s2b])
            k.red(tmp[:], sq[:, 6:10, :].rearrange("p c t -> p t c"), reads=st.ab, writes=[tmpb])
            rq, rqb = st.stats_rstd(s2[:], QLR, RMS_EPS, [s2b], "rstd")
            rkv, rkvb = st.stats_rstd(tmp[:], KVLR, RMS_EPS, [tmpb], "r2")
            for c in range(10):
                g_ap = gq_s[:, c:c + 1] if c < 6 else gkv_s[:, c - 6:c - 5]
                r_t, r_b = (rq, rqb) if c < 6 else (rkv, rkvb)
                k.stt(cqn[:, c, :], cq[:, c, :], g_ap, r_t[:], ALU.mult, ALU.mult,
                      reads=st.ab + [gb_, r_b], writes=[cqnb[c]])
            slab, sbuf_ = k.load_slab(w_inA[26], KC, 128)
            rope_store(slab, sbuf_, KC, ht, hb, tsl, KPE[:, tsl])
            for s in range(16):
                slab, sbuf_ = k.load_slab(w_inA[10 + s], KC, 128)
                bk = bank_rot.next()
                for kc in range(KC):
                    k.mm(bk, slab[:, kc, :], ht[:, kc, :], reads=[sbuf_, hb[kc]], start=(kc == 0), stop=(kc == KC - 1))
                dst = SQ[:, s, tsl] if s < 8 else SK[:, s - 8, tsl]
                evac_store(bk, dst, use_act=(s % 2 == 0))
            for g in range(2):
                slab, sbuf_ = k.load_slab(w_inV[g], KC, 512)
                for ts_ in range(4):
                    bk = bank_rot.next()
                    for kc in range(KC):
                        k.mm(bk, ht[:, kc, ts_ * 128:(ts_ + 1) * 128], slab[:, kc, :], reads=[sbuf_, hb[kc]],
                             start=(kc == 0), stop=(kc == KC - 1))
                    r0 = t * TT + ts_ * 128
                    evac_store(bk, SV[r0:r0 + 128, g * 512:(g + 1) * 512], use_act=(ts_ % 2 == 0))
            for h in range(8):
                slab, sbuf_ = k.load_slab(w_uq[2 * h], 6, 128)
                bk = bank_rot.next()
                for kc in range(6):
                    k.mm(bk, slab[:, kc, :], cqn[:, kc, :], reads=[sbuf_, cqnb[kc]], start=(kc == 0), stop=(kc == 5))
                evac_store(bk, QN[:, h, tsl], use_act=(h % 2 == 0))
                slab, sbuf_ = k.load_slab(w_uq[2 * h + 1], 6, 128)
                rope_store(slab, sbuf_, 6, cqn[:, 0:6, :], cqnb[0:6], tsl, QPE[:, h, tsl])
            ckvn = cqn[:, 6:10, :]
            ckvb = cqnb[6:10]
            for h in range(8):
                slab, sbuf_ = k.load_slab(w_ukvK[h], 4, 128)
                bk = bank_rot.next()
                for kc in range(4):
                    k.mm(bk, slab[:, kc, :], ckvn[:, kc, :], reads=[sbuf_, ckvb[kc]], start=(kc == 0), stop=(kc == 3))
                evac_store(bk, KN[:, h, tsl], use_act=(h % 2 == 1))
            for g in range(2):
                slab, sbuf_ = k.load_slab(w_ukvV[g], 4, 512)
                for ts_ in range(4):
                    bk = bank_rot.next()
                    for kc in range(4):
                        k.mm(bk, ckvn[:, kc, ts_ * 128:(ts_ + 1) * 128], slab[:, kc, :], reads=[sbuf_, ckvb[kc]],
                             start=(kc == 0), stop=(kc == 3))
                    r0 = t * TT + ts_ * 128
                    evac_store(bk, VM[r0:r0 + 128, g * 512:(g + 1) * 512], use_act=(ts_ % 2 == 1))
        k.P.emit(es)
    return nc


NQT = SEQ // TT
NKB = SEQ // 128
HPC = 4


def build_s2(nheads=HPC, nqt=NQT):
    nc = bass.Bass("TRN2", target_bir_lowering=False)
    with ExitStack() as es:
        k = KB(nc, es, slab_elems=64, nslab=1)
        P = k.P
        QN = k.din("QN", [128, HPC, SEQ], BF16)
        QPE = k.din("QPE", [64, HPC, SEQ], BF16)
        KN = k.din("KN", [128, HPC, SEQ], BF16)
        KPE = k.din("KPE", [64, SEQ], BF16)
        VH = k.din("VH", [128, HPC, NKB, 128], BF16)
        SQ = k.din("SQ", [128, HPC, SEQ], BF16)
        SK = k.din("SK", [128, HPC, SEQ], BF16)
        SVH = k.din("SVH", [128, HPC, NKB, 128], BF16)
        MI = k.din("mask_incl", [128, 4, TT], BF16)
        MS = k.din("mask_strict", [128, 4, TT], BF16)
        TRI = k.din("tri", [128, 3, 128], BF16)
        OM = k.dout("OM", [128, HPC, SEQ], F32)
        OS = k.dout("OS", [128, HPC, SEQ], F32)

        mi = k.sb("mi", [128, 4, TT], BF16)
        ms = k.sb("ms", [128, 4, TT], BF16)
        tri = k.sb("tri", [128, 3, 128], BF16)
        cb = Buf("consts")
        k.dma_in(mi[:], MI, writes=[cb])
        k.dma_in(ms[:], MS, writes=[cb])
        k.dma_in(tri[:], TRI, writes=[cb])
        kpe = k.sb("kpe", [64, SEQ], BF16)
        kpeb = Buf("kpe")
        k.dma_in(kpe[:], KPE, writes=[kpeb])
        hb = Rot([dict(q=k.sb("hq%d" % i, [128, SEQ], BF16), qpe=k.sb("hqpe%d" % i, [64, SEQ], BF16),
                       kk=k.sb("hk%d" % i, [128, SEQ], BF16), v=k.sb("hv%d" % i, [128, NKB, 128], BF16),
                       buf=Buf("head%d" % i)) for i in range(2)])
        pt = Rot([(k.sb("pt%d" % i, [128, TT], BF16), Buf("pt%d" % i)) for i in range(4)])
        wt = Rot([(k.sb("wt%d" % i, [128, TT], BF16), Buf("wt%d" % i)) for i in range(3)])
        et = Rot([(k.sb("et%d" % i, [128, TT], F32), Buf("et%d" % i)) for i in range(2)])
        spt = Rot([(k.sb("spt%d" % i, [128, TT], F32), Buf("spt%d" % i)) for i in range(3)])
        argt = Rot([(k.sb("argt%d" % i, [128, TT], F32), Buf("argt%d" % i)) for i in range(2)])
        ot = Rot([(k.sb("ot%d" % i, [128, TT], F32), Buf("ot%d" % i)) for i in range(2)])
        rl = (k.sb("rl", [128, TT], F32), Buf("rl"))
        sbank = Rot(k.banks[0:3])
        obank = Rot(k.banks[3:5])
        lbank = Rot(k.banks[5:7])
        ones_b = tri[:, 0, :]
        U_b = tri[:, 1, :]
        L_b = tri[:, 2, :]

        for h in range(nheads):
            H = hb.next()
            hbuf = H["buf"]
            k.dma_in(H["q"][:], QN[:, h, :], writes=[hbuf])
            k.dma_in(H["qpe"][:], QPE[:, h, :], writes=[hbuf])
            k.dma_in(H["kk"][:], KN[:, h, :], writes=[hbuf])
            k.dma_in(H["v"][:], VH[:, h, :, :], writes=[hbuf])
            for qi in range(nqt):
                qsl = slice(qi * TT, (qi + 1) * TT)
                nkb = 4 * (qi + 1)
                ob = obank.next()
                lb = lbank.next()

                def S(kb):
                    bk = sbank.next()
                    ksl = slice(kb * 128, (kb + 1) * 128)
                    k.mm(bk, H["kk"][:, ksl], H["q"][:, qsl], reads=[hbuf], start=True, stop=False)
                    k.mm(bk, kpe[:, ksl], H["qpe"][:, qsl], reads=[hbuf, kpeb], start=False, stop=True)
                    return bk

                nxt = S(0)
                for kb in range(nkb):
                    cur = nxt
                    if kb + 1 < nkb:
                        nxt = S(kb + 1)
                    p_t, p_b = pt.next()
                    k.act(p_t[:], cur[0][:], AF.Exp, reads=[cur[1]], writes=[p_b], scale=MLA_SCALE)
                    r = kb - 4 * qi
                    if r >= 0:
                        k.tt(p_t[:], p_t[:], mi[:, r, :], ALU.mult, reads=[p_b, cb], writes=[p_b])
                    k.mm(ob, H["v"][:, kb, :], p_t[:], reads=[hbuf, p_b], start=(kb == 0), stop=(kb == nkb - 1))
                    k.mm(lb, ones_b, p_t[:], reads=[cb, p_b], start=(kb == 0), stop=(kb == nkb - 1))
                k.recip(rl[0][:], lb[0][:], reads=[lb[1]], writes=[rl[1]])
                o_t, o_b = ot.next()
                k.tt(o_t[:], ob[0][:], rl[0][:], ALU.mult, reads=[ob[1], rl[1]], writes=[o_b])
                k.dma_out(OM[:, h, qsl], o_t[:], reads=[o_b])

        for h in range(nheads):
            H = hb.next()
            hbuf = H["buf"]
            k.dma_in(H["q"][:], SQ[:, h, :], writes=[hbuf])
            k.dma_in(H["kk"][:], SK[:, h, :], writes=[hbuf])
            k.dma_in(H["v"][:], SVH[:, h, :, :], writes=[hbuf])
            for qi in range(nqt):
                qsl = slice(qi * TT, (qi + 1) * TT)
                nkb = 4 * (qi + 1)
                ob = obank.next()
                cbk = lbank.next()

                def Z(kb):
                    bk = sbank.next()
                    ksl = slice(kb * 128, (kb + 1) * 128)
                    k.mm(bk, H["kk"][:, ksl], H["q"][:, qsl], reads=[hbuf], start=True, stop=True)
                    return bk

                nxt = Z(nkb - 1)
                for idx, kb in enumerate(range(nkb - 1, -1, -1)):
                    cur = nxt
                    if kb - 1 >= 0:
                        nxt = Z(kb - 1)
                    r = kb - 4 * qi
                    e_t, e_b = et.next()
                    sp_t, sp_b = spt.next()
                    w_t, w_b = wt.next()
                    a_t, a_b = pt.next()
                    g_t, g_b = argt.next()
                    k.act(e_t[:], cur[0][:], AF.Exp, reads=[cur[1]], writes=[e_b], scale=-SB_SCALE)
                    k.act(sp_t[:], e_t[:], AF.Ln, reads=[e_b], writes=[sp_b], bias=1.0)
                    k.stt(w_t[:], cur[0][:], SB_SCALE, sp_t[:], ALU.mult, ALU.add, reads=[cur[1], sp_b], writes=[w_b])
                    if r >= 0:
                        k.tt(w_t[:], w_t[:], ms[:, r, :], ALU.mult, reads=[w_b, cb], writes=[w_b])
                    k.mm(cbk, U_b, w_t[:], reads=[cb, w_b], start=(idx == 0), stop=(kb == 0))
                    k.tt(g_t[:], cbk[0][:], sp_t[:], ALU.add, reads=[cbk[1], sp_b], writes=[g_b])
                    k.act(a_t[:], g_t[:], AF.Exp, reads=[g_b], writes=[a_b], scale=-1.0)
                    if r >= 0:
                        k.tt(a_t[:], a_t[:], ms[:, r, :], ALU.mult, reads=[a_b, cb], writes=[a_b])
                    if kb > 0:
                        k.mm(cbk, L_b, w_t[:], reads=[cb, w_b], start=False, stop=False)
                    k.mm(ob, H["v"][:, kb, :], a_t[:], reads=[hbuf, a_b], start=(idx == 0), stop=(kb == 0))
                o_t, o_b = ot.next()
                k.act(o_t[:], ob[0][:], AF.Identity, reads=[ob[1]], writes=[o_b])
                k.dma_out(OS[:, h, qsl], o_t[:], reads=[o_b])
        k.P.emit(es)
    return nc


def build_s3(ntile=NTILE):
    nc = bass.Bass("TRN2", target_bir_lowering=False)
    with ExitStack() as es:
        k = KB(nc, es)
        x1T = k.din("x1T", [128, KC, NTOK], F32)
        OT = k.din("OT", [128, KC, NTOK], F32)
        ada = k.din("ada", [128, 9, KC], F32)
        lnp = k.din("lnp", [128, 3, 2, KC], F32)
        gout = k.din("gout", [128, KC], F32)
        w_o = k.din("w_o", [KC, 128, KC, 128], F32)
        wi = k.din("wi", [FC, 128, KC, 256], F32)
        wo = k.din("wo", [KC, 128, FC, 128], F32)
        x3T = k.dout("x3T", [128, KC, NTOK], F32)

        st = TokStage(k)
        st.load_params(ada, lnp)
        gout_s = k.sb("gout_s", [128, KC], F32)
        gob = Buf("gout")
        k.dma_in(gout_s[:], gout, writes=[gob])
        ot = k.sb("ot", [128, KC, TT], F32)
        otb = Buf("ot")
        bank_rot = Rot(k.banks[0:6])
        for t in range(ntile):
            tsl = slice(t * TT, (t + 1) * TT)
            k.dma_in(st.xt[:], x1T[:, :, tsl], writes=st.xb)
            k.dma_in(ot[:], OT[:, :, tsl], writes=[otb])
            sq = st.atf[:, 0:KC * TT].rearrange("p (c t) -> p c t", c=KC)
            k.act(sq, ot[:], AF.Square, reads=[otb], writes=st.ab)
            s2, s2b = st.stat["s2"]
            tmp, tmpb = st.stat["tmp"]
            k.red(s2[:], sq[:, 0:8, :].rearrange("p c t -> p t c"), reads=st.ab, writes=[s2b])
            k.red(tmp[:], sq[:, 8:16, :].rearrange("p c t -> p t c"), reads=st.ab, writes=[tmpb])
            rm, rmb = st.stats_rstd(s2[:], 1024, RMS_EPS, [s2b], "rstd")
            rs, rsb = st.stats_rstd(tmp[:], 1024, RMS_EPS, [tmpb], "r2")
            for c in range(KC):
                r_t, r_b = (rm, rmb) if c < 8 else (rs, rsb)
                k.stt(st.ht[:, c, :], ot[:, c, :], gout_s[:, c:c + 1], r_t[:], ALU.mult, ALU.mult,
                      reads=[otb, gob, r_b], writes=[st.hb[c]])
            for i in range(KC):
                slab, sbuf_ = k.load_slab(w_o[i], KC, 128)
                bk = bank_rot.next()
                for kc in range(KC):
                    k.mm(bk, slab[:, kc, :], st.ht[:, kc, :], reads=[sbuf_, st.hb[kc]],
                         start=(kc == 0), stop=(kc == KC - 1))
                k.stt(st.xt[:, i, :], bk[0][:], st.ada[:, 5, i:i + 1], st.xt[:, i, :], ALU.mult, ALU.add,
                      reads=[bk[1], st.adab, st.xb[i]], writes=[st.xb[i]])
            st.layernorm(1)
            st.ffn(wi, wo, 6, 7, 8)
            st.layernorm(2)
            k.dma_out(x3T[:, :, tsl], st.xt[:], reads=st.xb)
        k.P.emit(es)
    return nc


ADA_N = 9 * D
ADA_PC = ADA_N // 8
ADA_TILES = [(0, 512), (512, 512), (1024, 512), (1536, 512), (2048, 256)]


def build_s0():
    nc = bass.Bass("TRN2", target_bir_lowering=False)
    with ExitStack() as es:
        k = KB(nc, es, slab_elems=KC * 512, nslab=3)
        cT = k.din("cT", [128, KC, NB], F32)
        ada_w = k.din("ada_w", [DEPTH, D, ADA_PC], F32)
        ada_b = k.din("ada_b", [DEPTH, ADA_PC], F32)
        out = k.dout("ada_out", [DEPTH, NB, ADA_PC], F32)
        c_s = k.sb("c_s", [128, KC, NB], F32)
        c_a = k.sb("c_a", [128, KC, NB], BF16)
        cbuf = Buf("c")
        k.dma_in(c_s[:], cT, writes=[cbuf])
        k.act(c_a[:], c_s[:], AF.Silu, reads=[cbuf], writes=[cbuf])
        row = k.sb("row", [NB, ADA_PC], F32)
        bias = k.sb("bias", [NB, ADA_PC], F32)
        rowb = Buf("row")
        biasb = Buf("bias")
        bank_rot = Rot(k.banks[0:4])
        for l in range(DEPTH):
            ba = ada_b[l:l + 1, :]
            k.dma_in(bias[:], bass.AP(ba.tensor, ba.offset, [[0, NB], [1, ADA_PC]]), writes=[biasb])
            wl = ada_w[l].rearrange("(kc p) n -> p kc n", p=128)
            for (c0, cn) in ADA_TILES:
                csl = slice(c0, c0 + cn)
                slab, sbuf_ = k.load_slab(wl[:, :, csl], KC, cn)
                bk = bank_rot.next()
                for kc in range(KC):
                    k.mm(bk, c_a[:, kc, :], slab[:, kc, :], reads=[cbuf, sbuf_],
                         start=(kc == 0), stop=(kc == KC - 1), out=bk[0][0:NB, 0:cn])
                k.tt(row[:, csl], bk[0][0:NB, 0:cn], bias[:, csl], ALU.add, reads=[bk[1], biasb], writes=[rowb])
            k.dma_out(out[l], row[:], reads=[rowb])
        k.P.emit(es)
    return nc


NCORES = 8
_BUILT = {}


def _prog(name, fn):
    if name not in _BUILT:
        _BUILT[name] = fn()
    return _BUILT[name]


def _launch(name, fn, in_maps, ncores=NCORES):
    nc = _prog(name, fn)
    res = run_bass_kernel_spmd(nc, in_maps, core_ids=list(range(ncores)))
    return res.results


def _c(a):
    return np.ascontiguousarray(a)


def _fm(a):
    T, n = a.shape[0], a.shape[1] // 128
    return _c(a.reshape(T, n, 128).transpose(2, 1, 0))


def _unfm(a):
    return _c(a.transpose(2, 1, 0)).reshape(a.shape[2], -1)


def _vec_fm(v):
    return _c(v.reshape(-1, 128).T)


def _slabs(w, cols_list):
    K = w.shape[0]
    out = []
    for cols in cols_list:
        sub = w[:, cols]
        out.append(sub.reshape(K // 128, 128, -1).transpose(1, 0, 2))
    return _c(np.stack(out, axis=0))


_SWAP = np.concatenate([np.arange(32, 64), np.arange(0, 32)])


def _prep_layer(l, ffn_wi, ffn_wo):
    wi = ffn_wi[l]
    g = wi[:, :FF].reshape(KC, 128, FC, 128)
    u = wi[:, FF:].reshape(KC, 128, FC, 128)
    wi_r = _c(np.concatenate([g, u], axis=3).transpose(2, 1, 0, 3))
    wo_r = _c(ffn_wo[l].reshape(FC, 128, KC, 128).transpose(2, 1, 0, 3))
    return wi_r, wo_r


def _consts():
    p = np.arange(128)[:, None, None]
    r = np.arange(4)[None, :, None]
    f = np.arange(TT)[None, None, :]
    mi = (128 * r + p <= f).astype(ml_dtypes.bfloat16)
    ms = (128 * r + p < f).astype(ml_dtypes.bfloat16)
    j = np.arange(128)[:, None]
    s = np.arange(128)[None, :]
    tri = np.stack([np.ones((128, 128)), (j > s), (j <= s)], axis=1).astype(ml_dtypes.bfloat16)
    half = 32
    invf = (np.float32(10000.0) ** (-np.arange(half, dtype=np.float32) / np.float32(half))).astype(np.float32)
    invf2 = np.concatenate([invf, invf]).reshape(64, 1).astype(np.float32)
    return _c(mi), _c(ms), _c(tri), invf2


def _forward(x, c, positions, ada_w, ada_b, ln_g, ln_b, ffn1_wi, ffn1_wo, w_in, q_norm_g, kv_norm_g,
             w_uq, w_ukv, mla_out_g, sb_out_g, w_o, ffn2_wi, ffn2_wo, nlayers=DEPTH, debug=None):
    x = np.asarray(x, np.float32)
    mi, ms, tri, invf2 = _consts()
    cT = _c(np.asarray(c, np.float32).reshape(NB, KC, 128).transpose(2, 1, 0))
    in_maps = []
    for i in range(NCORES):
        sl = slice(i * ADA_PC, (i + 1) * ADA_PC)
        in_maps.append({"cT": cT, "ada_w": _c(ada_w[:, :, sl]), "ada_b": _c(ada_b[:, sl])})
    r0 = _launch("s0", build_s0, in_maps)
    ada_full = np.concatenate([r["ada_out"] for r in r0], axis=2)
    if debug is not None:
        debug["ada"] = ada_full

    def core_tok(i):
        return i // 2, slice((i % 2) * NTOK, (i % 2 + 1) * NTOK)

    xT = []
    for i in range(NCORES):
        b, tsl = core_tok(i)
        xT.append(_fm(x[b, tsl]))
    pos = [_c(np.asarray(positions[core_tok(i)[0], core_tok(i)[1]], np.int32)) for i in range(NCORES)]

    for l in range(nlayers):
        ada_l = [_c(ada_full[l, b].reshape(9, KC, 128).transpose(2, 0, 1)) for b in range(NB)]
        lnp = _c(np.stack([ln_g[l].reshape(3, KC, 128).transpose(2, 0, 1),
                           ln_b[l].reshape(3, KC, 128).transpose(2, 0, 1)], axis=2))
        wi1, wo1 = _prep_layer(l, ffn1_wi, ffn1_wo)
        win = w_in[l]
        cols = [np.arange(s * 128, (s + 1) * 128) for s in range(10)]
        cols += [1344 + h * 128 + np.arange(128) for h in range(16)]
        cols += [np.concatenate([1280 + np.arange(64), 1280 + _SWAP])]
        w_inA = _slabs(win, cols)
        w_inV = _slabs(win, [1344 + 2048 + g * 512 + np.arange(512) for g in range(2)])
        uq_cols = []
        for h in range(8):
            uq_cols.append(h * 192 + np.arange(128))
            uq_cols.append(np.concatenate([h * 192 + 128 + np.arange(64), h * 192 + 128 + _SWAP]))
        w_uq_r = _slabs(w_uq[l], uq_cols)
        w_ukvK = _slabs(w_ukv[l], [h * 256 + np.arange(128) for h in range(8)])
        w_ukvV = _slabs(w_ukv[l], [np.concatenate([h * 256 + 128 + np.arange(128) for h in range(4 * g, 4 * g + 4)])
                                   for g in range(2)])
        gq = _vec_fm(q_norm_g[l])
        gkv = _vec_fm(kv_norm_g[l])
        in_maps = []
        for i in range(NCORES):
            b, _ = core_tok(i)
            in_maps.append({"xT": xT[i], "ada": ada_l[b], "lnp": lnp, "wi": wi1, "wo": wo1, "w_inA": w_inA,
                            "w_inV": w_inV, "w_uq": w_uq_r, "w_ukvK": w_ukvK, "w_ukvV": w_ukvV, "gq": gq,
                            "gkv": gkv, "pos": pos[i], "invf": invf2})
        r1 = _launch("s1", build_s1, in_maps)
        del wi1, wo1, in_maps
        if debug is not None:
            debug["s1_%d" % l] = r1
        in_maps = []
        for i in range(NCORES):
            b, hh = i // 2, i % 2
            hs = slice(hh * HPC, (hh + 1) * HPC)
            ra, rb = r1[2 * b], r1[2 * b + 1]

            def cat_fm(nm):
                return _c(np.concatenate([ra[nm][:, hs, :], rb[nm][:, hs, :]], axis=2))

            def tokmajor(nm):
                full = np.concatenate([ra[nm], rb[nm]], axis=0)
                v = full.reshape(NKB, 128, 8, 128)[:, :, hs, :]
                return _c(v.transpose(1, 2, 0, 3))

            in_maps.append({"QN": cat_fm("QN"), "QPE": cat_fm("QPE"), "KN": cat_fm("KN"),
                            "KPE": _c(np.concatenate([ra["KPE"], rb["KPE"]], axis=1)),
                            "VH": tokmajor("VM"), "SQ": cat_fm("SQ"), "SK": cat_fm("SK"), "SVH": tokmajor("SV"),
                            "mask_incl": mi, "mask_strict": ms, "tri": tri})
        r2 = _launch("s2", build_s2, in_maps)
        if debug is not None:
            debug["s2_%d" % l] = r2
        wi2, wo2 = _prep_layer(l, ffn2_wi, ffn2_wo)
        w_o_r = _c(w_o[l].reshape(KC, 128, KC, 128).transpose(2, 1, 0, 3))
        gout = _vec_fm(np.concatenate([mla_out_g[l], sb_out_g[l]]))
        in_maps = []
        for i in range(NCORES):
            b, half = i // 2, i % 2
            tsl = slice(half * NTOK, (half + 1) * NTOK)
            ra, rb = r2[2 * b], r2[2 * b + 1]
            OT = _c(np.concatenate([ra["OM"][:, :, tsl], rb["OM"][:, :, tsl],
                                    ra["OS"][:, :, tsl], rb["OS"][:, :, tsl]], axis=1))
            in_maps.append({"x1T": r1[i]["x1T"], "OT": OT, "ada": ada_l[b], "lnp": lnp, "gout": gout,
                            "w_o": w_o_r, "wi": wi2, "wo": wo2})
        r3 = _launch("s3", build_s3, in_maps)
        del wi2, wo2, in_maps
        if debug is not None:
            debug["s3_%d" % l] = r3
        xT = [r3[i]["x3T"] for i in range(NCORES)]

    out = np.empty((NB, SEQ, D), np.float32)
    for i in range(NCORES):
        b, tsl = core_tok(i)
        out[b, tsl] = _unfm(xT[i])
    return out


def kernel(**inputs):
    return _forward(**inputs)
```
